# Optimizing a Trainium2 kernel written in Bass

```python
import math
import jax, jax.numpy as jnp
from jax import lax
import numpy as np

D_MODEL = 1024
BATCH = 4
SEQ = 8192
DEPTH = 2

GRID_W = 64
CTX_LEN = 256
N_MIXERS = 2
D_FF = 4 * D_MODEL
NORM_EPS = 1e-6
LRU_WIDTH = D_MODEL
LRU_HEADS = 4
LRU_BLOCK = LRU_WIDTH // LRU_HEADS
LRU_CONV = 4
LRU_CONV_LEFT = 2
LRU_C = 8.0
N_DIRS = 2
HYENA_ORDER = 2
HYENA_CONV = 3
HYENA_CONV_LEFT = 1
FILTER_BANDS = 16
FILTER_EMB = 1 + 2 * FILTER_BANDS
FILTER_HIDDEN = 64
FILTER_TARGET = 1e-2
FAST_DECAY_PCT = 0.3
SLOW_DECAY_PCT = 1.5

kernel_name = 'hybrid_rglru_hyena_diffusion_block'


def rms_norm(x, g):
    x32 = x.astype(jnp.float32)
    y = x32 * lax.rsqrt(jnp.mean(x32 * x32, axis=-1, keepdims=True) + NORM_EPS)
    return (y * g.astype(jnp.float32)).astype(x.dtype)


def modulate(x, g, shift, scale):
    return rms_norm(x, g) * (1 + scale) + shift


def squared_relu_mlp(u, w1, w2):
    return jnp.square(jax.nn.relu(u @ w1)) @ w2


def depthwise_conv(u, w, b, left):
    K = w.shape[0]
    L = u.shape[1]
    up = jnp.pad(u, ((0, 0), (left, K - 1 - left), (0, 0)))
    return sum(up[:, k:k + L] * w[k] for k in range(K)) + b


def row_conv(u, w, b, left):
    B, L, C = u.shape
    rows = L // GRID_W
    y = depthwise_conv(u.reshape(B * rows, GRID_W, C), w, b, left)
    return y.reshape(B, L, C)


def _combine(left, right):
    a1, b1 = left
    a2, b2 = right
    return a1 * a2, a2 * b1 + b2


def linear_scan(a, bx, h0):
    a_cum, h = lax.associative_scan(_combine, (a, bx), axis=1)
    return h + a_cum * h0[:, None]


def rglru_coeffs(xc, w_a, b_a, w_i, b_i, lam):
    B, L, W = xc.shape
    x32 = xc.astype(jnp.float32)
    xh = x32.reshape(B, L, LRU_HEADS, LRU_BLOCK)
    r = jax.nn.sigmoid(jnp.einsum('blhi,ehij->eblhj', xh, w_a) + b_a[:, None, None]).reshape(N_DIRS, B, L, W)
    i = jax.nn.sigmoid(jnp.einsum('blhi,ehij->eblhj', xh, w_i) + b_i[:, None, None]).reshape(N_DIRS, B, L, W)
    log_a = -LRU_C * r * jax.nn.softplus(-lam.astype(jnp.float32))[:, None, None, :]
    a = jnp.exp(log_a)
    bx = jnp.sqrt(-jnp.expm1(2.0 * log_a)) * i * x32[None]
    return a, bx


def rglru_mixer(u, u_ctx, p, want_ctx_out):
    w_in, b_in, conv_w, conv_b, w_a, b_a, w_i, b_i, lam, w_out, b_out = p
    W = LRU_WIDTH
    B = u.shape[0]
    zeros = jnp.zeros((B, W), jnp.float32)
    xc_c = depthwise_conv(u_ctx @ w_in[:, W:] + b_in[W:], conv_w, conv_b, LRU_CONV_LEFT)
    a_c, bx_c = rglru_coeffs(xc_c, w_a, b_a, w_i, b_i, lam)
    h_cf = linear_scan(a_c[0], bx_c[0], zeros)
    h_cb = linear_scan(jnp.flip(a_c[1], 1), jnp.flip(bx_c[1], 1), zeros)
    z = u @ w_in + b_in
    gate = jax.nn.gelu(z[..., :W])
    xl = row_conv(z[..., W:], conv_w, conv_b, LRU_CONV_LEFT)
    a_l, bx_l = rglru_coeffs(xl, w_a, b_a, w_i, b_i, lam)
    h_f = linear_scan(a_l[0], bx_l[0], h_cf[:, -1])
    h_b = jnp.flip(linear_scan(jnp.flip(a_l[1], 1), jnp.flip(bx_l[1], 1), h_cb[:, -1]), 1)
    y = ((h_f + h_b).astype(gate.dtype) * gate) @ w_out + b_out
    y_ctx = None
    if want_ctx_out:
        gate_c = jax.nn.gelu(u_ctx @ w_in[:, :W] + b_in[:W])
        h_c = h_cf + jnp.flip(h_cb, 1)
        y_ctx = (h_c.astype(gate_c.dtype) * gate_c) @ w_out + b_out
    return y, y_ctx


def hyena_filters(L, fw1, fb1, fw2, fb2, fw3, fb3, fw4, freq):
    t = jnp.linspace(0.0, 1.0, L, dtype=jnp.float32)[:, None]
    w = (2.0 * math.pi / L) * jnp.arange(L, dtype=jnp.float32)[:, None]
    bands = jnp.linspace(1e-4, FILTER_BANDS - 1, FILTER_BANDS, dtype=jnp.float32)
    pos = jnp.concatenate([t, jnp.cos(bands * w), -jnp.sin(bands * w)], axis=-1)
    h = jnp.sin(freq * (pos @ fw1 + fb1))
    h = jnp.sin(freq * (h @ fw2 + fb2))
    h = jnp.sin(freq * (h @ fw3 + fb3))
    h = (h @ fw4).reshape(L, N_DIRS, HYENA_ORDER, D_MODEL)
    deltas = jnp.abs(jnp.linspace(math.log(FILTER_TARGET) / SLOW_DECAY_PCT,
                                  math.log(FILTER_TARGET) / FAST_DECAY_PCT, D_MODEL, dtype=jnp.float32))
    h = h * jnp.exp(-t * deltas)[:, None, None, :]
    h = h / jnp.sum(jnp.abs(h), axis=(0, 1), keepdims=True)
    fwd, bwd = h[:, 0], h[:, 1]
    k = jnp.concatenate([fwd[:1] + bwd[:1], fwd[1:],
                         jnp.zeros((1, HYENA_ORDER, D_MODEL), h.dtype), jnp.flip(bwd[1:], 0)], axis=0)
    return jnp.fft.rfft(k, axis=0)


def long_conv(u, k_f, bias):
    L = u.shape[1]
    u32 = u.astype(jnp.float32)
    y = jnp.fft.irfft(jnp.fft.rfft(u32, n=2 * L, axis=1) * k_f, n=2 * L, axis=1)[:, :L]
    return (y + u32 * bias).astype(u.dtype)


def hyena_operator(u, p, conv_fn):
    w_in, b_in, conv_w, conv_b, fw1, fb1, fw2, fb2, fw3, fb3, fw4, freq, skip, w_out, b_out = p
    L = u.shape[1]
    z = conv_fn(u @ w_in + b_in, conv_w, conv_b, HYENA_CONV_LEFT)
    v, x1, x2 = jnp.split(z, HYENA_ORDER + 1, axis=-1)
    k_f = hyena_filters(L, fw1, fb1, fw2, fb2, fw3, fb3, fw4, freq)
    v = x1 * long_conv(v, k_f[:, 0], skip[0])
    v = x2 * long_conv(v, k_f[:, 1], skip[1])
    return v @ w_out + b_out


def _dense(key, shape, fan_in, gain=1.0):
    return (gain * fan_in ** -0.5) * jax.random.normal(key, shape, jnp.float32)


def setup_inputs(seed: int = 0) -> dict:
    key = jax.random.key(seed)
    k = list(jax.random.split(key, 40))
    D, W, F = D_MODEL, LRU_WIDTH, FILTER_HIDDEN
    n_a = (DEPTH + 1) // 2
    n_b = DEPTH // 2
    small = lambda kk, shape: 0.02 * jax.random.normal(kk, shape, jnp.float32)
    s = jax.random.uniform(k[14], (n_a, N_DIRS, W), jnp.float32, minval=0.9, maxval=0.999) ** (1.0 / LRU_C)
    return {
        'x': jax.random.normal(k[0], (BATCH, SEQ, D), jnp.float32),
        'c': jax.random.normal(k[1], (BATCH, D), jnp.float32),
        'ctx': jax.random.normal(k[2], (BATCH, CTX_LEN, D), jnp.float32),
        'c_ctx': jax.random.normal(k[3], (D,), jnp.float32),
        'ada_w': _dense(k[4], (DEPTH, D, 6 * D), D, 0.5),
        'ada_b': small(k[5], (DEPTH, 6 * D)),
        'norm_g': 1.0 + 0.1 * jax.random.normal(k[6], (DEPTH, 2, D), jnp.float32),
        'mlp_w1': _dense(k[7], (DEPTH, D, D_FF), D),
        'mlp_w2': _dense(k[8], (DEPTH, D_FF, D), D_FF),
        'lru_w_in': _dense(k[9], (n_a, D, 2 * W), D),
        'lru_b_in': small(k[10], (n_a, 2 * W)),
        'lru_conv_w': _dense(k[11], (n_a, LRU_CONV, W), LRU_CONV),
        'lru_conv_b': small(k[12], (n_a, W)),
        'lru_w_a': _dense(k[13], (n_a, N_DIRS, LRU_HEADS, LRU_BLOCK, LRU_BLOCK), LRU_BLOCK),
        'lru_b_a': small(k[15], (n_a, N_DIRS, LRU_HEADS, LRU_BLOCK)),
        'lru_w_i': _dense(k[16], (n_a, N_DIRS, LRU_HEADS, LRU_BLOCK, LRU_BLOCK), LRU_BLOCK),
        'lru_b_i': small(k[17], (n_a, N_DIRS, LRU_HEADS, LRU_BLOCK)),
        'lru_lambda': jnp.log(s) - jnp.log1p(-s),
        'lru_w_out': _dense(k[18], (n_a, W, D), W),
        'lru_b_out': small(k[19], (n_a, D)),
        'hy_w_in': _dense(k[20], (n_b, D, (HYENA_ORDER + 1) * D), D),
        'hy_b_in': small(k[21], (n_b, (HYENA_ORDER + 1) * D)),
        'hy_conv_w': _dense(k[22], (n_b, HYENA_CONV, (HYENA_ORDER + 1) * D), HYENA_CONV),
        'hy_conv_b': small(k[23], (n_b, (HYENA_ORDER + 1) * D)),
        'hy_fw1': _dense(k[24], (n_b, FILTER_EMB, F), FILTER_EMB),
        'hy_fb1': small(k[25], (n_b, F)),
        'hy_fw2': _dense(k[26], (n_b, F, F), F),
        'hy_fb2': small(k[27], (n_b, F)),
        'hy_fw3': _dense(k[28], (n_b, F, F), F),
        'hy_fb3': small(k[29], (n_b, F)),
        'hy_fw4': _dense(k[30], (n_b, F, N_DIRS * HYENA_ORDER * D), F),
        'hy_freq': 1.0 + 0.01 * jax.random.normal(k[31], (n_b, F), jnp.float32),
        'hy_skip': jax.random.normal(k[32], (n_b, HYENA_ORDER, D), jnp.float32),
        'hy_w_out': _dense(k[33], (n_b, D, D), D),
        'hy_b_out': small(k[34], (n_b, D)),
        'final_g': 1.0 + 0.1 * jax.random.normal(k[35], (D,), jnp.float32),
    }


def reference(x, c, ctx, c_ctx, ada_w, ada_b, norm_g, mlp_w1, mlp_w2,
              lru_w_in, lru_b_in, lru_conv_w, lru_conv_b, lru_w_a, lru_b_a, lru_w_i, lru_b_i,
              lru_lambda, lru_w_out, lru_b_out,
              hy_w_in, hy_b_in, hy_conv_w, hy_conv_b, hy_fw1, hy_fb1, hy_fw2, hy_fb2,
              hy_fw3, hy_fb3, hy_fw4, hy_freq, hy_skip, hy_w_out, hy_b_out, final_g):
    cond = jax.nn.silu(c)
    cond_ctx = jax.nn.silu(c_ctx)
    h_ctx = ctx
    for i in range(DEPTH):
        j = i // N_MIXERS
        is_lru = (i % N_MIXERS) == 0
        ctx_later = any(l % N_MIXERS == 0 for l in range(i + 1, DEPTH))
        mod = (cond @ ada_w[i] + ada_b[i])[:, None, :]
        sh1, sc1, g1, sh2, sc2, g2 = jnp.split(mod, 6, axis=-1)
        u = modulate(x, norm_g[i, 0], sh1, sc1)
        u_ctx = None
        mod_c = None
        if is_lru or ctx_later:
            mod_c = jnp.split(cond_ctx @ ada_w[i] + ada_b[i], 6)
            u_ctx = modulate(h_ctx, norm_g[i, 0], mod_c[0], mod_c[1])
        if is_lru:
            p = (lru_w_in[j], lru_b_in[j], lru_conv_w[j], lru_conv_b[j], lru_w_a[j], lru_b_a[j],
                 lru_w_i[j], lru_b_i[j], lru_lambda[j], lru_w_out[j], lru_b_out[j])
            y, y_ctx = rglru_mixer(u, u_ctx, p, ctx_later)
        else:
            p = (hy_w_in[j], hy_b_in[j], hy_conv_w[j], hy_conv_b[j], hy_fw1[j], hy_fb1[j],
                 hy_fw2[j], hy_fb2[j], hy_fw3[j], hy_fb3[j], hy_fw4[j], hy_freq[j], hy_skip[j],
                 hy_w_out[j], hy_b_out[j])
            y = hyena_operator(u, p, row_conv)
            y_ctx = hyena_operator(u_ctx, p, depthwise_conv) if ctx_later else None
        x = x + g1 * y
        x = x + g2 * squared_relu_mlp(modulate(x, norm_g[i, 1], sh2, sc2), mlp_w1[i], mlp_w2[i])
        if ctx_later:
            h_ctx = h_ctx + mod_c[2] * y_ctx
            h_ctx = h_ctx + mod_c[5] * squared_relu_mlp(
                modulate(h_ctx, norm_g[i, 1], mod_c[3], mod_c[4]), mlp_w1[i], mlp_w2[i])
    return rms_norm(x, final_g)
```

```python
import math
import numpy as np
import ml_dtypes
import concourse.bass as bass
import concourse.mybir as mybir
from concourse.bass_utils import run_bass_kernel_spmd
from contextlib import ExitStack

F32 = mybir.dt.float32
BF16 = mybir.dt.bfloat16
AF = mybir.ActivationFunctionType
ALU = mybir.AluOpType

SAME_ENGINE_SYNC = True
NDS = 24
D = 1024
EPS = 1e-6

class R:
    __slots__ = ("w", "rs", "name")

    def __init__(self, name=""):
        self.w = None
        self.rs = {}
        self.name = name


class KB:
    def __init__(self, nc, es):
        self.nc, self.es = nc, es
        self.eng = dict(pe=nc.tensor, act=nc.scalar, dve=nc.vector, pool=nc.gpsimd, sp=nc.sync)
        self.sem = {e: es.enter_context(nc.semaphore("sem_" + e)) for e in ["pe", "act", "dve", "pool"]}
        self.cnt = {e: 0 for e in self.sem}
        self.dsems = [es.enter_context(nc.semaphore("dsem%d" % i)) for i in range(NDS)]
        self.dtarget = [0] * NDS
        self.dnext = 0
        self.waited = {e: {} for e in self.eng}
        self.nuniq = 0

    def sb(self, shape, dt, name=None):
        self.nuniq += 1
        return self.es.enter_context(self.nc.sbuf_tensor("%s_%d" % (name or "sb", self.nuniq), list(shape), dt))

    def ps(self, shape, dt=F32, name=None):
        self.nuniq += 1
        return self.es.enter_context(self.nc.psum_tensor("%s_%d" % (name or "ps", self.nuniq), list(shape), dt))

    def _semof(self, key):
        return self.sem[key] if isinstance(key, str) else self.dsems[key[1]]

    def _wait(self, e, key, val):
        if key == e and (e == "pe" or not SAME_ENGINE_SYNC):
            return
        if self.waited[e].get(key, 0) >= val:
            return
        self.eng[e].wait_ge(self._semof(key), val)
        self.waited[e][key] = val

    def _deps(self, e, reads, writes):
        for r in reads:
            if r.w is not None:
                self._wait(e, *r.w)
        for w in writes:
            if w.w is not None:
                self._wait(e, *w.w)
            for k, v in w.rs.items():
                self._wait(e, k, v)

    def _record(self, tok, reads, writes):
        k, v = tok
        for r in reads:
            r.rs[k] = v
        for w in writes:
            w.w = tok
            w.rs = {}

    def op(self, e, reads, writes, fn):
        self._deps(e, reads, writes)
        ins = fn(self.eng[e])
        self.cnt[e] += 1
        ins.then_inc(self.sem[e], 1)
        self._record((e, self.cnt[e]), reads, writes)
        return ins

    def dma(self, q, out, in_, reads, writes, **kw):
        self._deps(q, reads, writes)
        i = self.dnext
        self.dnext = (i + 1) % NDS
        if self.dtarget[i] > 0:
            self._wait(q, ("d", i), self.dtarget[i])
        self.dtarget[i] += 16
        self.eng[q].dma_start(out=out, in_=in_, **kw).then_inc(self.dsems[i], 16)
        self._record((("d", i), self.dtarget[i]), reads, writes)

    def finish(self, outs):
        for r in outs:
            if r.w is not None:
                self._wait("sp", *r.w)

    def mm(self, out, lhsT, rhs, reads, writes, start=True, stop=True):
        return self.op("pe", reads, writes, lambda g: g.matmul(out, lhsT, rhs, start=start, stop=stop))

    def act(self, out, in_, func, reads, writes, bias=None, scale=None, e="act"):
        kw = {}
        if bias is not None:
            kw["bias"] = bias
        if scale is not None:
            kw["scale"] = scale
        return self.op("act", reads, writes, lambda g: g.activation(out=out, in_=in_, func=func, **kw))

    def tt(self, e, out, in0, in1, op, reads, writes):
        return self.op(e, reads, writes, lambda g: g.tensor_tensor(out=out, in0=in0, in1=in1, op=op))

    def ts(self, e, out, in0, s1, s2, op0, op1, reads, writes):
        if op1 is None:
            return self.op(e, reads, writes, lambda g: g.tensor_scalar(out=out, in0=in0, scalar1=s1, scalar2=None, op0=op0))
        return self.op(e, reads, writes, lambda g: g.tensor_scalar(out=out, in0=in0, scalar1=s1, scalar2=s2, op0=op0, op1=op1))

    def stt(self, out, in0, scalar, in1, op0, op1, reads, writes):
        return self.op("dve", reads, writes, lambda g: g.scalar_tensor_tensor(out=out, in0=in0, scalar=scalar, in1=in1, op0=op0, op1=op1))

    def copy(self, e, out, in_, reads, writes):
        if e == "act":
            return self.op(e, reads, writes, lambda g: g.copy(out=out, in_=in_))
        return self.op(e, reads, writes, lambda g: g.tensor_copy(out=out, in_=in_))

    def memset(self, e, ap, val, writes):
        return self.op(e, [], writes, lambda g: g.memset(ap, val))

    def barrier(self):
        for e in self.eng:
            for k in self.sem:
                if self.cnt[k] > 0:
                    self._wait(e, k, self.cnt[k])
            for i in range(NDS):
                if self.dtarget[i] > 0:
                    self._wait(e, ("d", i), self.dtarget[i])

    def push_scope(self):
        st = ExitStack()
        self._scopes = getattr(self, "_scopes", [])
        self._scopes.append(self.es)
        self.es = st
        return st

    def pop_scope(self):
        self.barrier()
        self.es.close()
        self.es = self._scopes.pop()

    def init_psum(self):
        self.pbanks = [(self.ps([128, 512]), R("psum%d" % i)) for i in range(8)]
        self.pnext = 0

    def next_ps(self):
        p = self.pbanks[self.pnext]
        self.pnext = (self.pnext + 1) % len(self.pbanks)
        return p

    def load_w(self, dst, dst_r, src, K, F, col0=0):
        step = 1024
        for k in range(K):
            for f0 in range(0, F, step):
                f1 = min(F, f0 + step)
                self.dma("pool", dst[:, k, f0:f1], src[k * 128:(k + 1) * 128, col0 + f0:col0 + f1], [], [dst_r])

    def matvec(self, W, W_r, nf, vec, vec_r, nv, ps, ps_r, col0=0, K=8, fofs=0):
        for f in range(nf):
            for k in range(K):
                self.mm(ps[:, col0 + f * nv:col0 + (f + 1) * nv], W[:, k, (fofs + f) * 128:(fofs + f + 1) * 128],
                        vec[:, k, 0:nv], [W_r, vec_r], [ps_r], start=(k == 0), stop=(k == K - 1))


def _dr(nc, n, s, dt=F32):
    return nc.dram_tensor(n, list(s), dt, kind="ExternalInput").ap()


def _rstd(kb, ps, ps_r, n, epsb, r_eps, out, out_r):
    kb.act(out, ps, AF.Ln, [ps_r, r_eps], [out_r], bias=epsb[:])
    kb.act(out, out, AF.Exp, [out_r], [out_r], scale=-0.5)


def build_mlp(final, NT=4096, TT=256, dbg=False):
    nc = bass.Bass("TRN2", target_bir_lowering=False)
    if dbg:
        d_mod = nc.dram_tensor("d_mod", [128, 32], F32, kind="ExternalOutput").ap()
        d_shw1 = nc.dram_tensor("d_shw1", [128, 32], F32, kind="ExternalOutput").ap()
        d_x1 = nc.dram_tensor("d_x1", [128, 8, TT], F32, kind="ExternalOutput").ap()
        d_u = nc.dram_tensor("d_u", [128, 8, TT], BF16, kind="ExternalOutput").ap()
    xT = _dr(nc, "xT", [D, NT]); gT = _dr(nc, "gT", [D, NT], BF16); cv = _dr(nc, "cv", [128, 8])
    ada = _dr(nc, "ada", [D, 4096]); adab = _dr(nc, "adab", [128, 32]); ng = _dr(nc, "ng", [128, 8])
    wout = _dr(nc, "wout", [D, D]); bout = _dr(nc, "bout", [128, 8]); w1 = _dr(nc, "w1", [D, 4096]); w2 = _dr(nc, "w2", [4096, D])
    fg = _dr(nc, "fg", [128, 8])
    oT = nc.dram_tensor("oT", [D, NT], F32, kind="ExternalOutput").ap()
    xTv = xT.rearrange("(k p) t -> p k t", p=128)
    gTv = gT.rearrange("(k p) t -> p k t", p=128)
    oTv = oT.rearrange("(k p) t -> p k t", p=128)
    with ExitStack() as es:
        kb = KB(nc, es); kb.init_psum()
        ones = kb.sb([128, 128], BF16); r_ones = R(); kb.memset("pool", ones[:], 1.0 / D, [r_ones])
        epsb = kb.sb([128, 1], F32); r_eps = R(); kb.memset("pool", epsb[:], EPS, [r_eps])
        sv = kb.sb([128, 5, 8], F32); r_sv = R()
        kb.dma("sp", sv[:, 0, :], cv[:, :], [], [r_sv]); kb.dma("sp", sv[:, 1, :], ng[:, :], [], [r_sv])
        kb.dma("sp", sv[:, 2, :], bout[:, :], [], [r_sv]); kb.dma("sp", sv[:, 3, :], fg[:, :], [], [r_sv])
        adb = kb.sb([128, 32], F32); r_adb = R(); kb.dma("sp", adb[:], adab[:, :], [], [r_adb])
        cond = kb.sb([128, 8, 1], BF16); r_cond = R()
        kb.act(cond[:, :, 0], sv[:, 0, :], AF.Silu, [r_sv], [r_cond])
        wo_sb = kb.sb([128, 8, D], BF16); r_wo = R(); kb.load_w(wo_sb, r_wo, wout, 8, D)
        slab = kb.sb([128, 8, 256], BF16); r_slab = R()
        mod = kb.sb([128, 32], F32); r_mod = R()
        psm, r_psm = kb.next_ps()
        for m in range(16):
            kb.load_w(slab, r_slab, ada, 8, 256, col0=m * 256)
            kb.matvec(slab, r_slab, 2, cond, r_cond, 1, psm, r_psm, col0=m * 2)
        kb.tt("dve", mod[:], psm[:, 0:32], adb[:], ALU.add, [r_psm, r_adb], [r_mod])
        w1_sb = kb.sb([128, 8, 4096], BF16); r_w1 = R(); kb.load_w(w1_sb, r_w1, w1, 8, 4096)
        w2_sb = kb.sb([128, 32, D], BF16); r_w2 = R(); kb.load_w(w2_sb, r_w2, w2, 32, D)
        g1 = mod[:, 0:8]; sh2 = mod[:, 8:16]; sc2 = mod[:, 16:24]; g2 = mod[:, 24:32]
        dv = kb.sb([128, 3, 8], F32); r_dv = R()
        kb.stt(dv[:, 0, :], sc2, 1.0, sv[:, 1, :], ALU.add, ALU.mult, [r_mod, r_sv], [r_dv])
        kb.tt("dve", dv[:, 1, :], g1, sv[:, 2, :], ALU.mult, [r_mod, r_sv], [r_dv])
        sh2b = kb.sb([128, 8, 1], BF16); r_sh2b = R()
        kb.copy("dve", sh2b[:, :, 0], sh2, [r_mod], [r_sh2b])
        shw1 = kb.sb([128, 32], F32); r_shw1 = R()
        psm2, r_psm2 = kb.next_ps()
        kb.matvec(w1_sb, r_w1, 32, sh2b, r_sh2b, 1, psm2, r_psm2)
        kb.copy("dve", shw1[:], psm2[:, 0:32], [r_psm2], [r_shw1])
        if dbg:
            kb.dma("sp", d_mod[:, :], mod[:], [r_mod], [R()])
            kb.dma("sp", d_shw1[:, :], shw1[:], [r_shw1], [R()])
        xt = [kb.sb([128, 8, TT], F32) for _ in range(2)]; r_xt = [R(), R()]
        gt0 = kb.sb([128, 8, TT], BF16); gt = [gt0, gt0]; r_gt0 = R(); r_gt = [r_gt0, r_gt0]
        x1 = kb.sb([128, 8, TT], F32); r_x1 = [R() for _ in range(8)]
        sq = kb.sb([128, 8, TT], BF16); r_sq = [R() for _ in range(8)]
        rstd = kb.sb([128, TT], F32); r_rstd = R()
        u = kb.sb([128, 8, TT], BF16); r_u = [R() for _ in range(8)]
        hr = [kb.sb([128, TT], F32) for _ in range(2)]; r_hr = [R(), R()]
        h = kb.sb([128, 32, TT], BF16); r_h = [R() for _ in range(32)]
        r_out = R()
        ntile = NT // TT

        def load(t):
            b = t % 2
            kb.dma("sp", xt[b][:], xTv[:, :, t * TT:(t + 1) * TT], [], [r_xt[b]])

        def load_g(t):
            b = t % 2
            kb.dma("sp", gt[b][:], gTv[:, :, t * TT:(t + 1) * TT], [], [r_gt[b]])

        def norm(src, r_src):
            for k in range(8):
                kb.act(sq[:, k, :], src[:, k, :], AF.Square, [r_src[k]], [r_sq[k]])
            pm, r_pm = kb.next_ps()
            for k in range(8):
                kb.mm(pm[:, 0:TT], ones[:], sq[:, k, :], [r_ones, r_sq[k]], [r_pm], start=(k == 0), stop=(k == 7))
            _rstd(kb, pm[:, 0:TT], r_pm, TT, epsb, r_eps, rstd[:], r_rstd)

        load(0)
        load_g(0)
        for t in range(ntile):
            b = t % 2
            if t + 1 < ntile:
                load(t + 1)
            xb = xt[b]; x2 = xt[b]; r_xb = [r_xt[b]] * 8; r_x2 = [r_xt[b]] * 8
            for k in range(8):
                kb.ts("pool", xb[:, k, :], xt[b][:, k, :], dv[:, 1, k:k + 1], None, ALU.add, None, [r_xt[b], r_dv], [r_xb[k]])
            for f in range(8):
                py, r_py = kb.next_ps()
                for k in range(8):
                    kb.mm(py[:, 0:TT], wo_sb[:, k, f * 128:(f + 1) * 128], gt[b][:, k, :], [r_wo, r_gt[b]], [r_py], start=(k == 0), stop=(k == 7))
                kb.stt(x1[:, f, :], py[:, 0:TT], mod[:, f:f + 1], xb[:, f, :], ALU.mult, ALU.add, [r_py, r_mod, r_xb[f]], [r_x1[f]])
            if t + 1 < ntile:
                load_g(t + 1)
            norm(x1, r_x1)
            for k in range(8):
                kb.stt(u[:, k, :], x1[:, k, :], dv[:, 0, k:k + 1], rstd[:], ALU.mult, ALU.mult, [r_x1[k], r_dv, r_rstd], [r_u[k]])
            if dbg and t == 0:
                kb.dma("sp", d_u[:, :, :], u[:], r_u, [R()])
                kb.dma("sp", d_x1[:, :, :], x1[:], r_x1, [R()])
            for f in range(32):
                ph, r_ph = kb.next_ps()
                for k in range(8):
                    kb.mm(ph[:, 0:TT], w1_sb[:, k, f * 128:(f + 1) * 128], u[:, k, :], [r_w1, r_u[k]], [r_ph], start=(k == 0), stop=(k == 7))
                j = f % 2
                kb.act(hr[j][:], ph[:, 0:TT], AF.Relu, [r_ph, r_shw1], [r_hr[j]], bias=shw1[:, f:f + 1])
                kb.tt("pool" if j else "dve", h[:, f, :], hr[j][:], hr[j][:], ALU.mult, [r_hr[j]], [r_h[f]])
            for f in range(8):
                po, r_po = kb.next_ps()
                for k in range(32):
                    kb.mm(po[:, 0:TT], w2_sb[:, k, f * 128:(f + 1) * 128], h[:, k, :], [r_w2, r_h[k]], [r_po], start=(k == 0), stop=(k == 31))
                kb.stt(x2[:, f, :], po[:, 0:TT], mod[:, 24 + f:25 + f], x1[:, f, :], ALU.mult, ALU.add, [r_po, r_mod, r_x1[f]], [r_x2[f]])
            if final:
                norm(x2, r_x2)
                for k in range(8):
                    kb.stt(x2[:, k, :], x2[:, k, :], sv[:, 3, k:k + 1], rstd[:], ALU.mult, ALU.mult, [r_x2[k], r_sv, r_rstd], [r_x2[k]])
            kb.dma("sp", oTv[:, :, t * TT:(t + 1) * TT], x2[:], [r_xt[b]], [r_out])
        kb.finish([r_out])
    return nc


def pk(v):
    v = np.asarray(v, np.float32)
    return np.ascontiguousarray(v.reshape(-1, 128).T)


def build_lru(NTOK=8192, TT=256, NCTX=256, ROW=64):
    assert NCTX == TT
    nc = bass.Bass("TRN2", target_bir_lowering=False)
    xT = _dr(nc, "xT", [D, NTOK]); ctxT = _dr(nc, "ctxT", [D, NCTX]); cv2 = _dr(nc, "cv2", [128, 8, 2])
    ada = _dr(nc, "ada", [D, 2048]); adab = _dr(nc, "adab", [128, 16]); ng = _dr(nc, "ng", [128, 8])
    wg = _dr(nc, "wg", [D, 512]); wr = _dr(nc, "wr", [D, 512]); bg = _dr(nc, "bg", [128, 4]); br = _dr(nc, "br", [128, 4])
    cw = _dr(nc, "cw", [128, 4, 4]); cb = _dr(nc, "cb", [128, 4])
    wa = _dr(nc, "wa", [1024, 256]); wi = _dr(nc, "wi", [1024, 256])
    ba = _dr(nc, "ba", [128, 2, 4]); bi = _dr(nc, "bi", [128, 2, 4]); lam = _dr(nc, "lam", [128, 2, 4])
    gT = nc.dram_tensor("gT", [512, NTOK], BF16, kind="ExternalOutput").ap()
    Psc = nc.dram_tensor("Psc", [512, NTOK], F32).ap()
    xTv = xT.rearrange("(k p) t -> p k t", p=128)
    cTv = ctxT.rearrange("(k p) t -> p k t", p=128)
    gTv = gT.rearrange("(k p) t -> p k t", p=128)
    Pv = Psc.rearrange("(k p) t -> p k t", p=128)
    with ExitStack() as es:
        kb = KB(nc, es); kb.init_psum()
        ones = kb.sb([128, 128], BF16); r_ones = R(); kb.memset("pool", ones[:], 1.0 / D, [r_ones])
        epsb = kb.sb([128, 1], F32); r_eps = R(); kb.memset("pool", epsb[:], EPS, [r_eps])
        oneb = kb.sb([128, 1], F32); r_oneb = R(); kb.memset("pool", oneb[:], 1.0, [r_oneb])
        r_sv = R()

        def small(src, shape):
            t_ = kb.sb(shape, F32)
            kb.dma("sp", t_[:], src, [], [r_sv])
            return t_
        cvt = small(cv2[:, :, :], [128, 8, 2]); ngt = small(ng[:, :], [128, 8]); adbt = small(adab[:, :], [128, 16])
        bgt = small(bg[:, :], [128, 4]); brt = small(br[:, :], [128, 4]); cwt = small(cw[:, :, :], [128, 4, 4]); cbt = small(cb[:, :], [128, 4])
        bat = small(ba[:, :, :], [128, 2, 4]); bit = small(bi[:, :, :], [128, 2, 4]); lamt = small(lam[:, :, :], [128, 2, 4])
        cond = kb.sb([128, 8, 2], BF16); r_cond = R()
        kb.act(cond[:], cvt[:], AF.Silu, [r_sv], [r_cond])
        slab = kb.sb([128, 8, 256], BF16); r_slab = R()
        psm, r_psm = kb.next_ps()
        for m in range(8):
            kb.load_w(slab, r_slab, ada, 8, 256, col0=m * 256)
            kb.matvec(slab, r_slab, 2, cond, r_cond, 2, psm, r_psm, col0=m * 4)
        mod = kb.sb([128, 16, 2], F32); r_mod = R()
        for j in range(2):
            kb.tt("dve", mod[:, :, j], psm[:, j:32:2], adbt[:], ALU.add, [r_psm, r_sv], [r_mod])
        gs = kb.sb([128, 8, 2], F32); r_gs = R()
        for j in range(2):
            kb.stt(gs[:, :, j], mod[:, 8:16, j], 1.0, ngt[:], ALU.add, ALU.mult, [r_mod, r_sv], [r_gs])
        shb = kb.sb([128, 8, 2], BF16); r_shb = R()
        kb.copy("dve", shb[:], mod[:, 0:8, :], [r_mod], [r_shb])
        wg_sb = kb.sb([128, 8, 512], BF16); r_wg = R(); kb.load_w(wg_sb, r_wg, wg, 8, 512)
        wr_sb = kb.sb([128, 8, 512], BF16); r_wr = R(); kb.load_w(wr_sb, r_wr, wr, 8, 512)
        wa_sb = kb.sb([128, 8, 256], BF16); r_wa = R(); kb.load_w(wa_sb, r_wa, wa, 8, 256)
        wi_sb = kb.sb([128, 8, 256], BF16); r_wi = R(); kb.load_w(wi_sb, r_wi, wi, 8, 256)
        psb, r_psb = kb.next_ps()
        kb.matvec(wg_sb, r_wg, 4, shb, r_shb, 2, psb, r_psb, col0=0)
        kb.matvec(wr_sb, r_wr, 4, shb, r_shb, 2, psb, r_psb, col0=8)
        biasg = kb.sb([128, 4, 2], F32); biasr = kb.sb([128, 4, 2], F32); r_bias = R()
        for j in range(2):
            kb.tt("dve", biasg[:, :, j], psb[:, j:8:2], bgt[:], ALU.add, [r_psb, r_sv], [r_bias])
            kb.tt("dve", biasr[:, :, j], psb[:, 8 + j:16:2], brt[:], ALU.add, [r_psb, r_sv], [r_bias])
        cl = kb.sb([128, 2, 4], F32); r_cl = R()
        kb.act(cl[:], lamt[:], AF.Exp, [r_sv], [r_cl], scale=-1.0)
        kb.act(cl[:], cl[:], AF.Ln, [r_cl, r_oneb], [r_cl], bias=oneb[:])
        kb.ts("dve", cl[:], cl[:], -8.0, None, ALU.mult, None, [r_cl], [r_cl])
        gate_all = kb.sb([128, 4, NTOK], BF16); xl_all = kb.sb([128, 4, NTOK], BF16)
        ntile = NTOK // TT
        r_gate = [R() for _ in range(ntile)]; r_xla = [R() for _ in range(ntile)]
        xt = kb.sb([128, 8, TT], F32); r_xt = R()
        sq = kb.sb([128, 8, TT], BF16); r_sq = [R() for _ in range(8)]
        rstd = kb.sb([128, TT], F32); r_rstd = R()
        u = kb.sb([128, 8, TT], BF16); r_u = [R() for _ in range(8)]
        zb = kb.sb([128, 4, TT], F32); r_zb = [R() for _ in range(4)]
        xl32 = kb.sb([128, 4, TT], F32); r_xl32 = [R() for _ in range(4)]
        tmp = [[kb.sb([128, TT], F32) for _ in range(4)] for _ in range(2)]
        r_tmp = [[R() for _ in range(4)] for _ in range(2)]
        hbuf = [kb.sb([128, 4, TT], F32) for _ in range(2)]; r_hb = [[R() for _ in range(4)] for _ in range(2)]
        pt0 = kb.sb([128, 4, TT], F32); pt = [pt0, pt0]; r_pt0 = R(); r_pt = [r_pt0, r_pt0]
        go0 = kb.sb([128, 4, TT], BF16); go = [go0, go0]; r_go0 = R(); r_go = [r_go0, r_go0]
        r_psc = [R() for _ in range(ntile)]
        r_out = R()

        def norm_u(src, r_src, j):
            for k in range(8):
                kb.act(sq[:, k, :], src[:, k, :], AF.Square, [r_src], [r_sq[k]])
            pm, r_pm = kb.next_ps()
            for k in range(8):
                kb.mm(pm[:, 0:TT], ones[:], sq[:, k, :], [r_ones, r_sq[k]], [r_pm], start=(k == 0), stop=(k == 7))
            _rstd(kb, pm[:, 0:TT], r_pm, TT, epsb, r_eps, rstd[:], r_rstd)
            for k in range(8):
                kb.stt(u[:, k, :], src[:, k, :], gs[:, k, j:j + 1], rstd[:], ALU.mult, ALU.mult, [r_src, r_gs, r_rstd], [r_u[k]])

        def proc(j, rowlen, gate_dst, r_gdst, xl_dst, r_xldst):
            if gate_dst is not None:
                for f in range(4):
                    pg, r_pg = kb.next_ps()
                    for k in range(8):
                        kb.mm(pg[:, 0:TT], wg_sb[:, k, f * 128:(f + 1) * 128], u[:, k, :], [r_wg, r_u[k]], [r_pg], start=(k == 0), stop=(k == 7))
                    kb.act(gate_dst[:, f, :], pg[:, 0:TT], AF.Gelu_apprx_tanh, [r_pg, r_bias], [r_gdst], bias=biasg[:, f, j:j + 1])
            nrow = TT // rowlen
            for f in range(4):
                pz, r_pz = kb.next_ps()
                for k in range(8):
                    kb.mm(pz[:, 0:TT], wr_sb[:, k, f * 128:(f + 1) * 128], u[:, k, :], [r_wr, r_u[k]], [r_pz], start=(k == 0), stop=(k == 7))
                kb.act(zb[:, f, :], pz[:, 0:TT], AF.Identity, [r_pz, r_bias], [r_zb[f]], bias=biasr[:, f, j:j + 1])
                kb.ts("pool", xl32[:, f, :], zb[:, f, :], cwt[:, 2, f:f + 1], cbt[:, f:f + 1], ALU.mult, ALU.add, [r_zb[f], r_sv], [r_xl32[f]])
                zv = zb[:, f, :].rearrange("p (r c) -> p r c", c=rowlen)
                xv = xl32[:, f, :].rearrange("p (r c) -> p r c", c=rowlen)
                kb.stt(xv[:, :, 2:], zv[:, :, 0:rowlen - 2], cwt[:, 0, f:f + 1], xv[:, :, 2:], ALU.mult, ALU.add, [r_zb[f], r_sv], [r_xl32[f]])
                kb.stt(xv[:, :, 1:], zv[:, :, 0:rowlen - 1], cwt[:, 1, f:f + 1], xv[:, :, 1:], ALU.mult, ALU.add, [r_zb[f], r_sv], [r_xl32[f]])
                kb.stt(xv[:, :, 0:rowlen - 1], zv[:, :, 1:], cwt[:, 3, f:f + 1], xv[:, :, 0:rowlen - 1], ALU.mult, ALU.add, [r_zb[f], r_sv], [r_xl32[f]])
                kb.copy("pool", xl_dst[:, f, :], xl32[:, f, :], [r_xl32[f]], [r_xldst])

        def dir_pass(e, xlb, r_xlb, xlf, r_xlf, inits, r_inits, reverse, hout, r_hout):
            for f in range(4):
                jh, fo = f // 2, f % 2
                tb = tmp[f % 2]; rt = r_tmp[f % 2]
                pa, r_pa = kb.next_ps()
                for kk in range(2):
                    kb.mm(pa[:, 0:TT], wa_sb[:, (e * 2 + jh) * 2 + kk, fo * 128:(fo + 1) * 128], xlb(jh * 2 + kk), [r_wa, r_xlb], [r_pa], start=(kk == 0), stop=(kk == 1))
                pi, r_pi = kb.next_ps()
                for kk in range(2):
                    kb.mm(pi[:, 0:TT], wi_sb[:, (e * 2 + jh) * 2 + kk, fo * 128:(fo + 1) * 128], xlb(jh * 2 + kk), [r_wi, r_xlb], [r_pi], start=(kk == 0), stop=(kk == 1))
                kb.act(tb[0][:], pa[:, 0:TT], AF.Sigmoid, [r_pa, r_sv], [rt[0]], bias=bat[:, e, f:f + 1])
                kb.act(tb[0][:], tb[0][:], AF.Exp, [rt[0], r_cl], [rt[0]], scale=cl[:, e, f:f + 1])
                kb.act(tb[1][:], pi[:, 0:TT], AF.Sigmoid, [r_pi, r_sv], [rt[1]], bias=bit[:, e, f:f + 1])
                kb.tt("pool", tb[2][:], tb[0][:], tb[0][:], ALU.mult, [rt[0]], [rt[2]])
                kb.act(tb[2][:], tb[2][:], AF.Sqrt, [rt[2], r_oneb], [rt[2]], bias=oneb[:], scale=-1.0)
                kb.tt("pool", tb[3][:], tb[2][:], tb[1][:], ALU.mult, [rt[2], rt[1]], [rt[3]])
                kb.tt("pool", tb[3][:], tb[3][:], xlf(f), ALU.mult, [rt[3], r_xlf[f] if isinstance(r_xlf, list) else r_xlf], [rt[3]])
                init = inits[f] if inits is not None else 0.0
                rd = [rt[0], rt[3]] + ([r_inits[f]] if inits is not None else [])
                if reverse:
                    kb.op("dve", rd, [r_hout[f]], lambda v, f=f, tb=tb, init=init: v.tensor_tensor_scan(
                        out=hout[:, f, ::-1], data0=tb[0][:, ::-1], data1=tb[3][:, ::-1], initial=init, op0=ALU.mult, op1=ALU.add))
                else:
                    kb.op("dve", rd, [r_hout[f]], lambda v, f=f, tb=tb, init=init: v.tensor_tensor_scan(
                        out=hout[:, f, :], data0=tb[0][:], data1=tb[3][:], initial=init, op0=ALU.mult, op1=ALU.add))

        kb.dma("sp", xt[:], cTv[:, :, :], [], [r_xt])
        norm_u(xt, r_xt, 1)
        xlc = xl_all[:, :, 0:TT]; r_xlc = r_xla[0]
        proc(1, NCTX, None, None, xlc, r_xlc)
        hcs = kb.sb([128, 2, 4], F32); r_hcs = R()
        for e in range(2):
            dir_pass(e, lambda c: xl_all[:, c, 0:TT], r_xlc, lambda f: xl32[:, f, :], r_xl32, None, None, e == 1, hbuf[e], r_hb[e])
            col = 0 if e == 1 else TT - 1
            kb.copy("pool", hcs[:, e, :], hbuf[e][:, :, col], r_hb[e], [r_hcs])
        hc = None
        kb.dma("sp", xt[:], xTv[:, :, 0:TT], [], [r_xt])
        for t in range(ntile):
            sl = slice(t * TT, (t + 1) * TT)
            norm_u(xt, r_xt, 0)
            if t + 1 < ntile:
                kb.dma("sp", xt[:], xTv[:, :, (t + 1) * TT:(t + 2) * TT], [], [r_xt])
            proc(0, ROW, gate_all[:, :, sl], r_gate[t], xl_all[:, :, sl], r_xla[t])
            hb_ = hbuf[t % 2]; rh = r_hb[t % 2]
            if t == 0:
                inits = [hcs[:, 0, f:f + 1] for f in range(4)]; r_in = [r_hcs] * 4
            else:
                inits = [hbuf[(t - 1) % 2][:, f, TT - 1:TT] for f in range(4)]; r_in = r_hb[(t - 1) % 2]
            dir_pass(0, lambda c, sl=sl: xl_all[:, c, sl], r_xla[t], lambda f: xl32[:, f, :], r_xl32, inits, r_in, False, hb_, rh)
            p_ = pt[t % 2]
            kb.tt("pool", p_[:], hb_[:], gate_all[:, :, sl], ALU.mult, rh + [r_gate[t]], [r_pt[t % 2]])
            kb.dma("sp", Pv[:, :, sl], p_[:], [r_pt[t % 2]], [r_psc[t]])
        pin = [xt[:, 0:4, :], xt[:, 4:8, :]]; r_pin = [r_xt, r_xt]
        kb.dma("sp", pin[(ntile - 1) % 2], Pv[:, :, (ntile - 1) * TT:ntile * TT], [r_psc[ntile - 1]], [r_pin[(ntile - 1) % 2]])
        for t in range(ntile - 1, -1, -1):
            sl = slice(t * TT, (t + 1) * TT)
            if t > 0:
                kb.dma("sp", pin[(t - 1) % 2], Pv[:, :, (t - 1) * TT:t * TT], [r_psc[t - 1]], [r_pin[(t - 1) % 2]])
            hb_ = hbuf[t % 2]; rh = r_hb[t % 2]
            if t == ntile - 1:
                inits = [hcs[:, 1, f:f + 1] for f in range(4)]; r_in = [r_hcs] * 4
            else:
                inits = [hbuf[(t + 1) % 2][:, f, 0:1] for f in range(4)]; r_in = r_hb[(t + 1) % 2]
            dir_pass(1, lambda c, sl=sl: xl_all[:, c, sl], r_xla[t], lambda f, sl=sl: xl_all[:, f, sl], r_xla[t], inits, r_in, True, hb_, rh)
            p_ = pt[t % 2]; g_ = go[t % 2]
            kb.tt("pool", p_[:], hb_[:], gate_all[:, :, sl], ALU.mult, rh + [r_gate[t]], [r_pt[t % 2]])
            kb.tt("dve", g_[:], p_[:], pin[t % 2], ALU.add, [r_pt[t % 2], r_pin[t % 2]], [r_go[t % 2]])
            kb.dma("sp", gTv[:, :, sl], g_[:], [r_go[t % 2]], [r_out])
        kb.finish([r_out])
    return nc


LSEQ = 8192
NFFT = 16384
MAGIC = 12582912.0


def hyena_consts():
    n = np.arange(128)
    k1 = np.arange(65)
    th = 2 * np.pi * np.outer(n, k1) / 128
    F1 = np.stack([np.cos(th), -np.sin(th)], 1)
    ph = -2 * np.pi * (n[:, None, None] * (k1[None, :, None] + 128 * n[None, None, :])) / NFFT
    G = np.stack([np.cos(ph), np.sin(ph), -np.sin(ph)], 1)
    th2 = 2 * np.pi * np.outer(n, n) / 128
    Finv = np.stack([np.stack([np.cos(th2), np.sin(th2)], 1), np.stack([-np.sin(th2), np.cos(th2)], 1)], 1)
    eps = np.full(65, 2.0); eps[0] = 1.0; eps[64] = 1.0
    ph3 = 2 * np.pi * k1[:, None, None] * (n[None, :, None] + 128 * np.arange(64)[None, None, :]) / NFFT
    T = np.stack([np.cos(ph3), -np.sin(ph3)], 1) * (eps / NFFT)[:, None, None, None]
    L = LSEQ
    t = np.linspace(0.0, 1.0, L, dtype=np.float32).astype(np.float64)
    w = (2.0 * math.pi / L) * np.arange(L, dtype=np.float32).astype(np.float64)
    bands = np.linspace(1e-4, 15, 16, dtype=np.float32).astype(np.float64)
    pos = np.concatenate([t[:, None], np.cos(bands * w[:, None]), -np.sin(bands * w[:, None])], -1)
    posR = np.concatenate([pos[:1], pos[:0:-1]], 0)
    tR = np.concatenate([[1e4], t[:0:-1]])
    tcol = -np.concatenate([t.reshape(64, 128), tR.reshape(64, 128)], 0)
    deltas = np.abs(np.linspace(math.log(1e-2) / 1.5, math.log(1e-2) / 0.3, D, dtype=np.float32))
    f32 = lambda a: np.ascontiguousarray(a, dtype=np.float32)
    return dict(F1=f32(F1.reshape(128, 130)), G=f32(G.reshape(128, 3 * 65 * 128)), Finv=f32(Finv.reshape(128, 512)),
                T=f32(T.reshape(65, 2 * 128 * 64)), posT=f32(np.stack([pos.T, posR.T], 0)), tcol=f32(tcol), deltas=deltas)


def build_hyena(NG=8, TT=256, ROW=64, dbg=False):
    GC = 64
    C = NG * GC
    NCH = C // 128
    NF = 3 * NCH
    L = LSEQ
    nc = bass.Bass("TRN2", target_bir_lowering=False)
    xT = _dr(nc, "xT", [D, L]); cv = _dr(nc, "cv", [128, 8]); ada = _dr(nc, "ada", [D, 2048]); adab = _dr(nc, "adab", [128, 16]); ng = _dr(nc, "ng", [128, 8])
    wz = _dr(nc, "wz", [D, 3 * C]); bz = _dr(nc, "bz", [128, NF]); cwz = _dr(nc, "cwz", [128, 3, NF]); cbz = _dr(nc, "cbz", [128, NF])
    fw1 = _dr(nc, "fw1", [33, 64]); fw2 = _dr(nc, "fw2", [64, 64]); fw3 = _dr(nc, "fw3", [64, 64]); fvec = _dr(nc, "fvec", [64, 4])
    fw4 = _dr(nc, "fw4", [64, 4 * C])
    skip = _dr(nc, "skip", [128, 2, NCH])
    cF1 = _dr(nc, "F1", [128, 130]); cG = _dr(nc, "G", [128, 3 * 65 * 128]); cFinv = _dr(nc, "Finv", [128, 512]); cT = _dr(nc, "T", [65, 2 * 128 * 64])
    posT = _dr(nc, "posT", [2, 33, L]); tcol = _dr(nc, "tcol", [128, 128]); deltas = _dr(nc, "deltas", [128, C])
    gT = nc.dram_tensor("gT", [C, L], BF16, kind="ExternalOutput").ap()
    ZT = nc.dram_tensor("ZT", [3, C, L], F32).ap()
    V1 = nc.dram_tensor("V1", [C, L], F32).ap()
    YT = nc.dram_tensor("YT", [C, L], BF16).ap()
    KfS = nc.dram_tensor("KfS", [2, NG, 128, 2 * 65 * GC], BF16).ap()
    RN = nc.dram_tensor("RN", [2, C, 1], F32).ap()
    if dbg:
        d_y = nc.dram_tensor("d_y", [C, L], BF16, kind="ExternalOutput").ap()
        d_rn = nc.dram_tensor("d_rn", [2, C, 1], F32, kind="ExternalOutput").ap()
        d_v = nc.dram_tensor("d_v", [C, L], F32, kind="ExternalOutput").ap()
    xTv = xT.rearrange("(k p) t -> p k t", p=128)
    with ExitStack() as es:
        kb = KB(nc, es); kb.init_psum()
        ones = kb.sb([128, 128], BF16); r_ones = R(); kb.memset("pool", ones[:], 1.0 / D, [r_ones])
        epsb = kb.sb([128, 1], F32); r_eps = R(); kb.memset("pool", epsb[:], EPS, [r_eps])
        onec = kb.sb([128, 1], F32); r_onec = R(); kb.memset("pool", onec[:], 1.0, [r_onec])
        F1_sb = kb.sb([128, 130], BF16); r_F1 = R(); kb.dma("pool", F1_sb[:], cF1[:, :], [], [r_F1])
        G_sb = kb.sb([128, 3, 65, 128], BF16); r_G = R()
        cGv = cG.rearrange("p (a k q) -> p a k q", a=3, k=65)
        for a in range(3):
            for k0 in range(0, 65, 13):
                kb.dma("pool", G_sb[:, a, k0:k0 + 13, :], cGv[:, a, k0:k0 + 13, :], [], [r_G])
        r_sv = R()

        def small(src, shape):
            t_ = kb.sb(shape, F32)
            kb.dma("sp", t_[:], src, [], [r_sv])
            return t_
        skt = small(skip[:, :, :], [128, 2, NCH])
        r_zt = [[R() for _ in range(L // TT)] for _ in range(3)]
        r_rn = R()
        kb.push_scope()
        h3 = kb.sb([64, 2, L], BF16); r_h3 = R()
        fw4_sb = kb.sb([64, 2, 2, C], BF16); r_fw4 = R()
        kb.dma("pool", fw4_sb[:], fw4.rearrange("f (d o c) -> f d o c", d=2, o=2), [], [r_fw4])
        fw1_sb = small(fw1[:, :], [33, 64]); fw2_sb = small(fw2[:, :], [64, 64]); fw3_sb = small(fw3[:, :], [64, 64]); fv = small(fvec[:, :], [64, 4])
        fbf = kb.sb([64, 3], F32); r_fbf = R()
        for i in range(3):
            kb.tt("dve", fbf[:, i:i + 1], fv[:, i:i + 1], fv[:, 3:4], ALU.mult, [r_sv], [r_fbf])
        kb.push_scope()
        FT = 512
        pin_ = [kb.sb([33, FT], F32) for _ in range(2)]; r_pin_ = [R(), R()]
        hh = [kb.sb([64, FT], F32) for _ in range(2)]; r_hh = [R(), R()]
        ta = kb.sb([64, FT], F32); r_ta = R()
        tb_ = kb.sb([64, FT], F32); r_tb = R()
        it = 0
        for d_ in range(2):
            for t in range(L // FT):
                p_ = pin_[it % 2]; rp = r_pin_[it % 2]; it += 1
                kb.dma("sp", p_[:], posT[d_, :, t * FT:(t + 1) * FT], [], [rp])
                src, r_src, K = p_, rp, 33
                for li, wsb in enumerate([fw1_sb, fw2_sb, fw3_sb]):
                    pf, r_pf = kb.next_ps()
                    kb.mm(pf[0:64, 0:FT], wsb[0:K, :], src[0:K, :], [r_sv, r_src], [r_pf])
                    kb.ts("dve", ta[:], pf[0:64, 0:FT], fv[:, 3:4], fbf[:, li:li + 1], ALU.mult, ALU.add, [r_pf, r_sv, r_fbf], [r_ta])
                    kb.ts("dve", tb_[:], ta[:], 1.0 / (2 * math.pi), MAGIC, ALU.mult, ALU.add, [r_ta], [r_tb])
                    kb.ts("dve", tb_[:], tb_[:], MAGIC, -2 * math.pi, ALU.subtract, ALU.mult, [r_tb], [r_tb])
                    kb.tt("dve", ta[:], ta[:], tb_[:], ALU.add, [r_ta, r_tb], [r_ta])
                    if li < 2:
                        dst, r_dst = hh[li], r_hh[li]
                        kb.act(dst[:], ta[:], AF.Sin, [r_ta], [r_dst])
                        src, r_src, K = dst, r_dst, 64
                    else:
                        kb.act(h3[:, d_, t * FT:(t + 1) * FT], ta[:], AF.Sin, [r_ta], [r_h3])
        kb.pop_scope()
        kb.push_scope()
        cvt = small(cv[:, :], [128, 8]); ngt = small(ng[:, :], [128, 8]); adbt = small(adab[:, :], [128, 16])
        bzt = small(bz[:, :], [128, NF]); cwt = small(cwz[:, :, :], [128, 3, NF]); cbt = small(cbz[:, :], [128, NF])
        cond = kb.sb([128, 8, 1], BF16); r_cond = R()
        kb.act(cond[:, :, 0], cvt[:], AF.Silu, [r_sv], [r_cond])
        slab = kb.sb([128, 8, 256], BF16); r_slab = R()
        psm, r_psm = kb.next_ps()
        for m in range(8):
            kb.load_w(slab, r_slab, ada, 8, 256, col0=m * 256)
            kb.matvec(slab, r_slab, 2, cond, r_cond, 1, psm, r_psm, col0=m * 2)
        mod = kb.sb([128, 16], F32); r_mod = R()
        kb.tt("dve", mod[:], psm[:, 0:16], adbt[:], ALU.add, [r_psm, r_sv], [r_mod])
        gs = kb.sb([128, 8], F32); r_gs = R()
        kb.stt(gs[:], mod[:, 8:16], 1.0, ngt[:], ALU.add, ALU.mult, [r_mod, r_sv], [r_gs])
        shb = kb.sb([128, 8, 1], BF16); r_shb = R()
        kb.copy("dve", shb[:, :, 0], mod[:, 0:8], [r_mod], [r_shb])
        wz_sb = kb.sb([128, 8, 3 * C], BF16); r_wz = R(); kb.load_w(wz_sb, r_wz, wz, 8, 3 * C)
        psb, r_psb = kb.next_ps()
        kb.matvec(wz_sb, r_wz, NF, shb, r_shb, 1, psb, r_psb)
        biasz = kb.sb([128, NF], F32); r_bias = R()
        kb.tt("dve", biasz[:], psb[:, 0:NF], bzt[:], ALU.add, [r_psb, r_sv], [r_bias])
        xt = kb.sb([128, 8, TT], F32); r_xt = R()
        sq = kb.sb([128, 8, TT], BF16); r_sq = [R() for _ in range(8)]
        rstd = kb.sb([128, TT], F32); r_rstd = R()
        u = kb.sb([128, 8, TT], BF16); r_u = [R() for _ in range(8)]
        zb = [kb.sb([128, TT], F32) for _ in range(2)]; r_zb = [R(), R()]
        zo = [kb.sb([128, NF, TT], F32) for _ in range(2)]; r_zo = [R(), R()]
        ntile = L // TT
        ZTv = ZT.rearrange("a (k p) t -> p a k t", p=128)
        kb.dma("sp", xt[:], xTv[:, :, 0:TT], [], [r_xt])
        for t in range(ntile):
            sl = slice(t * TT, (t + 1) * TT)
            for k in range(8):
                kb.act(sq[:, k, :], xt[:, k, :], AF.Square, [r_xt], [r_sq[k]])
            pm, r_pm = kb.next_ps()
            for k in range(8):
                kb.mm(pm[:, 0:TT], ones[:], sq[:, k, :], [r_ones, r_sq[k]], [r_pm], start=(k == 0), stop=(k == 7))
            _rstd(kb, pm[:, 0:TT], r_pm, TT, epsb, r_eps, rstd[:], r_rstd)
            for k in range(8):
                kb.stt(u[:, k, :], xt[:, k, :], gs[:, k:k + 1], rstd[:], ALU.mult, ALU.mult, [r_xt, r_gs, r_rstd], [r_u[k]])
            if t + 1 < ntile:
                kb.dma("sp", xt[:], xTv[:, :, (t + 1) * TT:(t + 2) * TT], [], [r_xt])
            zo_ = zo[t % 2]; rz = r_zo[t % 2]
            for f in range(NF):
                pz, r_pz = kb.next_ps()
                for k in range(8):
                    kb.mm(pz[:, 0:TT], wz_sb[:, k, f * 128:(f + 1) * 128], u[:, k, :], [r_wz, r_u[k]], [r_pz], start=(k == 0), stop=(k == 7))
                z_ = zb[f % 2]; rzb = r_zb[f % 2]
                kb.act(z_[:], pz[:, 0:TT], AF.Identity, [r_pz, r_bias], [rzb], bias=biasz[:, f:f + 1])
                kb.ts("pool", zo_[:, f, :], z_[:], cwt[:, 1, f:f + 1], cbt[:, f:f + 1], ALU.mult, ALU.add, [rzb, r_sv], [rz])
                zv = z_[:].rearrange("p (r c) -> p r c", c=ROW)
                xv = zo_[:, f, :].rearrange("p (r c) -> p r c", c=ROW)
                kb.stt(xv[:, :, 1:], zv[:, :, 0:ROW - 1], cwt[:, 0, f:f + 1], xv[:, :, 1:], ALU.mult, ALU.add, [rzb, r_sv], [rz])
                kb.stt(xv[:, :, 0:ROW - 1], zv[:, :, 1:], cwt[:, 2, f:f + 1], xv[:, :, 0:ROW - 1], ALU.mult, ALU.add, [rzb, r_sv], [rz])
            for a in range(3):
                kb.dma("sp", ZTv[:, a, :, sl], zo_[:, a * NCH:(a + 1) * NCH, :], [rz], [r_zt[a][t]])
        kb.pop_scope()
        r_ztall = [R() for _ in range(3)]
        kb.push_scope()
        dl = small(deltas[:, :], [128, C]); tc = small(tcol[:, :], [128, 128])
        Dk = [kb.sb([128, GC, 128], BF16) for _ in range(2)]; r_Dk = [R(), R()]
        acc = kb.sb([128, 2, GC], F32); r_acc = R()
        Ed = kb.sb([128, GC], F32); r_Ed = R()
        kt = kb.sb([128, 2, GC], F32); r_kt = R()
        ka = kb.sb([128, 2, GC], F32); r_ka = R()
        A = kb.sb([128, 2, 65, GC], BF16); r_A = R()
        Kfo = kb.sb([128, 2, 65, GC], BF16); r_Kfo = R()
        rnt = kb.sb([GC, 2], F32); r_rnt = R()
        RNv = RN.rearrange("o c x -> c (o x)")

        def fwd_stage1(Dt, r_Dt, K):
            for c in range(GC):
                if c % 3 == 0:
                    p1, r_p1 = kb.next_ps()
                kb.mm(p1[:, (c % 3) * 130:(c % 3 + 1) * 130], Dt[0:K, c, :], F1_sb[0:K, :], [r_Dt, r_F1], [r_p1])
                src = p1[:, (c % 3) * 130:(c % 3 + 1) * 130].rearrange("p (r k) -> p r k", r=2)
                kb.copy("act" if c % 2 else "dve", A[:, :, :, c], src, [r_p1], [r_A])

        def fwd_stage2(cb_):
            for kb0 in range(0, 65, 4):
                nk = min(4, 65 - kb0)
                px, r_px = kb.next_ps()
                X = px[:].rearrange("p (k r c) -> p k r c", k=4, r=2)
                for j in range(nk):
                    k1 = kb0 + j
                    kb.mm(X[:, j, 0, :], G_sb[:, 0, k1, :], A[:, 0, k1, :], [r_G, r_A], [r_px], start=True, stop=False)
                    kb.mm(X[:, j, 0, :], G_sb[:, 2, k1, :], A[:, 1, k1, :], [r_G, r_A], [r_px], start=False, stop=True)
                    kb.mm(X[:, j, 1, :], G_sb[:, 1, k1, :], A[:, 0, k1, :], [r_G, r_A], [r_px], start=True, stop=False)
                    kb.mm(X[:, j, 1, :], G_sb[:, 0, k1, :], A[:, 1, k1, :], [r_G, r_A], [r_px], start=False, stop=True)
                cb_(kb0, nk, X, r_px)

        for g in range(NG):
            cs = slice(g * GC, (g + 1) * GC)
            kb.memset("pool", acc[:], 0.0, [r_acc])
            for n2 in range(128):
                pk_, r_pk = kb.next_ps()
                PK = pk_[:, 0:2 * GC].rearrange("p (o c) -> p o c", o=2)
                kb.mm(PK[0:64, :, :], h3[:, 0, n2:L:128], fw4_sb[:, 0, :, cs], [r_h3, r_fw4], [r_pk])
                kb.mm(PK[64:128, :, :], h3[:, 1, n2:L:128], fw4_sb[:, 1, :, cs], [r_h3, r_fw4], [r_pk])
                kb.act(Ed[:], dl[:, cs], AF.Exp, [r_sv], [r_Ed], scale=tc[:, n2:n2 + 1])
                for o in range(2):
                    kb.tt("dve", kt[:, o, :], PK[:, o, :], Ed[:], ALU.mult, [r_pk, r_Ed], [r_kt])
                    kb.copy("act", Dk[o][:, :, n2], kt[:, o, :], [r_kt], [r_Dk[o]])
                    kb.act(ka[:, o, :], kt[:, o, :], AF.Abs, [r_kt], [r_ka])
                    kb.tt("pool", acc[:, o, :], acc[:, o, :], ka[:, o, :], ALU.add, [r_ka], [r_acc])
                if n2 == 0:
                    pb, r_pb = kb.next_ps()
                    PB = pb[0:1, 0:2 * GC].rearrange("p (o c) -> p o c", o=2)
                    kb.mm(PB, h3[:, 0, 0:1], fw4_sb[:, 1, :, cs], [r_h3, r_fw4], [r_pb])
                    for o in range(2):
                        kb.tt("dve", Dk[o][0:1, :, 0], Dk[o][0:1, :, 0], PB[:, o, :], ALU.add, [r_pb], [r_Dk[o]])
                        kb.act(ka[0:1, o, :], PB[:, o, :], AF.Abs, [r_pb], [r_ka])
                        kb.tt("pool", acc[0:1, o, :], acc[0:1, o, :], ka[0:1, o, :], ALU.add, [r_ka], [r_acc])
            pn, r_pn = kb.next_ps()
            for o in range(2):
                kb.mm(pn[0:GC, o:o + 1], acc[:, o, :], onec[:], [r_acc, r_onec], [r_pn])
            kb.op("dve", [r_pn], [r_rnt], lambda v: v.reciprocal(out=rnt[:], in_=pn[0:GC, 0:2]))
            for o in range(2):
                kb.dma("sp", RN[o, cs, :], rnt[:, o:o + 1], [r_rnt], [r_rn])
            for o in range(2):
                fwd_stage1(Dk[o], r_Dk[o], 128)

                def cbk(kb0, nk, X, r_X):
                    for ri in range(2):
                        kb.copy("act" if ri else "dve", Kfo[:, ri, kb0:kb0 + nk, :], X[:, 0:nk, ri, :], [r_X], [r_Kfo])
                fwd_stage2(cbk)
                kb.dma("sp", KfS[o, g, :, :], Kfo[:].rearrange("p r k c -> p (r k c)"), [r_Kfo], [R()])
        kb.pop_scope()
        kb.pop_scope()
        def conv_phase(o, src_ap, r_dummy):
            kb.push_scope()
            Finv_sb = kb.sb([128, 2, 256], BF16); r_Fi = R(); kb.dma("pool", Finv_sb[:], cFinv.rearrange("p (a n) -> p a n", a=2), [], [r_Fi])
            T_sb = kb.sb([65, 2, 128, 64], BF16); r_T = R()
            cTv = cT.rearrange("p (a e d) -> p a e d", a=2, e=128)
            for a in range(2):
                for e0 in range(0, 128, 16):
                    kb.dma("pool", T_sb[:, a, e0:e0 + 16, :], cTv[:, a, e0:e0 + 16, :], [], [r_T])
            Dd = kb.sb([64, GC, 128], BF16); r_Dd = R()
            A_ = kb.sb([128, 2, 65, GC], BF16)
            Kf = kb.sb([128, 2, 65, GC], BF16); r_Kf = R()
            Y = kb.sb([128, 2, GC, 65], BF16); r_Y = R()
            S = kb.sb([65, 2, 128, GC], BF16); r_S = R()
            O = kb.sb([64, GC, 128], BF16); r_O = R()
            tq = [kb.sb([128, 4, GC], F32) for _ in range(4)]; r_tq = [R() for _ in range(4)]
            nonlocal A, r_A
            A, r_A = A_, R()
            for g in range(NG):
                cs = slice(g * GC, (g + 1) * GC)
                srcv = src_ap[cs, :].rearrange("c (a b) -> a c b", b=128)
                for c0 in range(0, GC, 16):
                    kb.dma("pool", Dd[:, c0:c0 + 16, :], srcv[:, c0:c0 + 16, :], [], [r_Dd])
                kb.dma("sp", Kf[:].rearrange("p r k c -> p (r k c)"), KfS[o, g, :, :], [], [r_Kf])
                fwd_stage1(Dd, r_Dd, 64)

                def cbk(kb0, nk, X, r_X):
                    Kre = Kf[:, 0, kb0:kb0 + nk, :]; Kim = Kf[:, 1, kb0:kb0 + nk, :]
                    Xre = X[:, 0:nk, 0, :]; Xim = X[:, 0:nk, 1, :]
                    kb.tt("dve", tq[0][:, 0:nk, :], Xre, Kre, ALU.mult, [r_X, r_Kf], [r_tq[0]])
                    kb.tt("dve", tq[1][:, 0:nk, :], Xim, Kim, ALU.mult, [r_X, r_Kf], [r_tq[1]])
                    kb.tt("dve", tq[2][:, 0:nk, :], Xre, Kim, ALU.mult, [r_X, r_Kf], [r_tq[2]])
                    kb.tt("dve", tq[3][:, 0:nk, :], Xim, Kre, ALU.mult, [r_X, r_Kf], [r_tq[3]])
                    kb.tt("pool", Y[:, 0, :, kb0:kb0 + nk].rearrange("p c k -> p k c"), tq[0][:, 0:nk, :], tq[1][:, 0:nk, :], ALU.subtract, [r_tq[0], r_tq[1]], [r_Y])
                    kb.tt("pool", Y[:, 1, :, kb0:kb0 + nk].rearrange("p c k -> p k c"), tq[2][:, 0:nk, :], tq[3][:, 0:nk, :], ALU.add, [r_tq[2], r_tq[3]], [r_Y])
                fwd_stage2(cbk)
                for c in range(GC):
                    if c % 2 == 0:
                        p5, r_p5 = kb.next_ps()
                    o5 = p5[0:65, (c % 2) * 256:(c % 2 + 1) * 256]
                    kb.mm(o5, Y[:, 0, c, :], Finv_sb[:, 0, :], [r_Y, r_Fi], [r_p5], start=True, stop=False)
                    kb.mm(o5, Y[:, 1, c, :], Finv_sb[:, 1, :], [r_Y, r_Fi], [r_p5], start=False, stop=True)
                    kb.copy("act" if c % 2 else "dve", S[:, :, :, c], o5.rearrange("p (r e) -> p r e", r=2), [r_p5], [r_S])
                for e0 in range(0, 128, 8):
                    p6, r_p6 = kb.next_ps()
                    for j in range(8):
                        e = e0 + j
                        o6 = p6[0:64, j * GC:(j + 1) * GC]
                        kb.mm(o6, T_sb[:, 0, e, :], S[:, 0, e, :], [r_T, r_S], [r_p6], start=True, stop=False)
                        kb.mm(o6, T_sb[:, 1, e, :], S[:, 1, e, :], [r_T, r_S], [r_p6], start=False, stop=True)
                    kb.copy("act" if (e0 // 8) % 2 else "dve", O[:, :, e0:e0 + 8], p6[0:64, :].rearrange("p (e c) -> p c e", e=8), [r_p6], [r_O])
                kb.dma("sp", YT[cs, :].rearrange("c (a b) -> a c b", b=128), O[:], [r_O], [R()])
            kb.pop_scope()

        def gate_phase(o, vin_ap, xg_ap, out_ap, out_bf16):
            kb.push_scope()
            rn_sb = kb.sb([128, NCH], F32); r_rnsb = R()
            for k in range(NCH):
                kb.dma("sp", rn_sb[:, k:k + 1], RN[o, k * 128:(k + 1) * 128, :], [], [r_rnsb])
            GT_ = 512
            yt_ = [kb.sb([128, NCH, GT_], BF16) for _ in range(2)]; vt_ = [kb.sb([128, NCH, GT_], F32) for _ in range(2)]
            xg_ = [kb.sb([128, NCH, GT_], F32) for _ in range(2)]
            ob_ = [kb.sb([128, NCH, GT_], BF16 if out_bf16 else F32) for _ in range(2)]
            rr = [[R() for _ in range(4)] for _ in range(2)]
            YTv = YT.rearrange("(k p) t -> p k t", p=128); vv = vin_ap.rearrange("(k p) t -> p k t", p=128)
            xv_ = xg_ap.rearrange("(k p) t -> p k t", p=128); ov = out_ap.rearrange("(k p) t -> p k t", p=128)
            r_o = R()
            for t in range(L // GT_):
                b = t % 2; sl = slice(t * GT_, (t + 1) * GT_)
                kb.dma("sp", yt_[b][:], YTv[:, :, sl], [], [rr[b][0]])
                kb.dma("sp", vt_[b][:], vv[:, :, sl], [], [rr[b][1]])
                kb.dma("sp", xg_[b][:], xv_[:, :, sl], [], [rr[b][2]])
                for k in range(NCH):
                    kb.ts("pool", vt_[b][:, k, :], vt_[b][:, k, :], skt[:, o, k:k + 1], None, ALU.mult, None, [rr[b][1], r_sv], [rr[b][1]])
                    kb.stt(vt_[b][:, k, :], yt_[b][:, k, :], rn_sb[:, k:k + 1], vt_[b][:, k, :], ALU.mult, ALU.add, [rr[b][0], r_rnsb, rr[b][1]], [rr[b][1]])
                    kb.tt("pool", ob_[b][:, k, :], vt_[b][:, k, :], xg_[b][:, k, :], ALU.mult, [rr[b][1], rr[b][2]], [rr[b][3]])
                kb.dma("sp", ov[:, :, sl], ob_[b][:], [rr[b][3]], [r_o])
            kb.pop_scope()
            return r_o

        conv_phase(0, ZT[0], None)
        if dbg:
            kb.dma("sp", d_y[:, :], YT[:, :], [], [R()])
            kb.dma("sp", d_rn[:, :, :], RN[:, :, :], [], [R()])
            kb.dma("sp", d_v[:, :], ZT[0], [], [R()])
            kb.barrier()
        gate_phase(0, ZT[0], ZT[1], V1, False)
        conv_phase(1, V1, None)
        r_o = gate_phase(1, V1, ZT[2], gT, True)
        kb.barrier()
    return nc


_C = np.ascontiguousarray
_HC = None


def kernel(x, c, ctx, c_ctx, ada_w, ada_b, norm_g, mlp_w1, mlp_w2,
           lru_w_in, lru_b_in, lru_conv_w, lru_conv_b, lru_w_a, lru_b_a, lru_w_i, lru_b_i,
           lru_lambda, lru_w_out, lru_b_out,
           hy_w_in, hy_b_in, hy_conv_w, hy_conv_b, hy_fw1, hy_fb1, hy_fw2, hy_fb2,
           hy_fw3, hy_fb3, hy_fw4, hy_freq, hy_skip, hy_w_out, hy_b_out, final_g):
    global _HC
    f = lambda a: np.asarray(a, dtype=np.float32)
    x, c, ctx, c_ctx, ada_w, ada_b, norm_g, mlp_w1, mlp_w2 = map(f, (x, c, ctx, c_ctx, ada_w, ada_b, norm_g, mlp_w1, mlp_w2))
    lru_w_in, lru_b_in, lru_conv_w, lru_conv_b, lru_w_a, lru_b_a, lru_w_i, lru_b_i = map(f, (lru_w_in, lru_b_in, lru_conv_w, lru_conv_b, lru_w_a, lru_b_a, lru_w_i, lru_b_i))
    lru_lambda, lru_w_out, lru_b_out = map(f, (lru_lambda, lru_w_out, lru_b_out))
    hy_w_in, hy_b_in, hy_conv_w, hy_conv_b, hy_fw1, hy_fb1, hy_fw2, hy_fb2 = map(f, (hy_w_in, hy_b_in, hy_conv_w, hy_conv_b, hy_fw1, hy_fb1, hy_fw2, hy_fb2))
    hy_fw3, hy_fb3, hy_fw4, hy_freq, hy_skip, hy_w_out, hy_b_out, final_g = map(f, (hy_fw3, hy_fb3, hy_fw4, hy_freq, hy_skip, hy_w_out, hy_b_out, final_g))
    B, S = x.shape[0], x.shape[1]
    HALF = S // 2
    cores = [(b, h) for b in range(B) for h in range(2)]
    ids = list(range(len(cores)))
    xTs = [_C(x[b].T) for b in range(B)]

    nc1 = build_lru(NTOK=S)
    maps = []
    for (b, h) in cores:
        hs = slice(h * 512, (h + 1) * 512)
        rs = slice(D + h * 512, D + (h + 1) * 512)
        maps.append({
            "xT": xTs[b], "ctxT": _C(ctx[b].T), "cv2": _C(np.stack([pk(c[b]), pk(c_ctx)], -1)),
            "ada": _C(ada_w[0][:, :2048]), "adab": pk(ada_b[0][:2048]), "ng": pk(norm_g[0, 0]),
            "wg": _C(lru_w_in[0][:, hs]), "wr": _C(lru_w_in[0][:, rs]), "bg": pk(lru_b_in[0][hs]), "br": pk(lru_b_in[0][rs]),
            "cw": _C(np.stack([pk(lru_conv_w[0][k, hs]) for k in range(4)], 1)), "cb": pk(lru_conv_b[0][hs]),
            "wa": _C(lru_w_a[0][:, 2 * h:2 * h + 2].reshape(1024, 256)), "wi": _C(lru_w_i[0][:, 2 * h:2 * h + 2].reshape(1024, 256)),
            "ba": _C(np.stack([pk(lru_b_a[0][e, 2 * h:2 * h + 2].reshape(512)) for e in range(2)], 1)),
            "bi": _C(np.stack([pk(lru_b_i[0][e, 2 * h:2 * h + 2].reshape(512)) for e in range(2)], 1)),
            "lam": _C(np.stack([pk(lru_lambda[0][e, hs]) for e in range(2)], 1)),
        })
    r1 = run_bass_kernel_spmd(nc1, maps, core_ids=ids).results
    GT = [_C(np.concatenate([np.asarray(r1[2 * b + h]["gT"]) for h in range(2)], 0)) for b in range(B)]

    def mlp_launch(layer, final, xT_list, GT_list, wout, bout):
        ncm = build_mlp(final, NT=HALF)
        maps = []
        for (b, h) in cores:
            ts_ = slice(h * HALF, (h + 1) * HALF)
            maps.append({
                "xT": _C(xT_list[b][:, ts_]), "gT": _C(GT_list[b][:, ts_]), "cv": pk(c[b]),
                "ada": _C(ada_w[layer][:, 2048:]), "adab": pk(ada_b[layer][2048:]), "ng": pk(norm_g[layer, 1]),
                "wout": _C(wout), "bout": pk(bout), "w1": _C(mlp_w1[layer]), "w2": _C(mlp_w2[layer]), "fg": pk(final_g),
            })
        r = run_bass_kernel_spmd(ncm, maps, core_ids=ids).results
        return [_C(np.concatenate([np.asarray(r[2 * b + h]["oT"]) for h in range(2)], 1)) for b in range(B)]

    x1T = mlp_launch(0, False, xTs, GT, lru_w_out[0], lru_b_out[0])

    if _HC is None:
        _HC = hyena_consts()
    hc = _HC
    nc3 = build_hyena(NG=8)
    maps = []
    for (b, h) in cores:
        chs = np.arange(h * 512, (h + 1) * 512)
        cols = np.concatenate([a * D + chs for a in range(3)])
        fcols = np.concatenate([(d * 2 + o) * D + chs for d in range(2) for o in range(2)])
        maps.append({
            "xT": x1T[b], "cv": pk(c[b]), "ada": _C(ada_w[1][:, :2048]), "adab": pk(ada_b[1][:2048]), "ng": pk(norm_g[1, 0]),
            "wz": _C(hy_w_in[0][:, cols]), "bz": pk(hy_b_in[0][cols]),
            "cwz": _C(np.stack([pk(hy_conv_w[0][k, cols]) for k in range(3)], 1)), "cbz": pk(hy_conv_b[0][cols]),
            "fw1": _C(hy_fw1[0]), "fw2": _C(hy_fw2[0]), "fw3": _C(hy_fw3[0]),
            "fvec": _C(np.stack([hy_fb1[0], hy_fb2[0], hy_fb3[0], hy_freq[0]], 1)),
            "fw4": _C(hy_fw4[0][:, fcols]), "skip": _C(np.stack([pk(hy_skip[0][o, chs]) for o in range(2)], 1)),
            "F1": hc["F1"], "G": hc["G"], "Finv": hc["Finv"], "T": hc["T"], "posT": hc["posT"], "tcol": hc["tcol"],
            "deltas": _C(np.broadcast_to(hc["deltas"][chs][None], (128, 512))),
        })
    r3 = run_bass_kernel_spmd(nc3, maps, core_ids=ids).results
    GT2 = [_C(np.concatenate([np.asarray(r3[2 * b + h]["gT"]) for h in range(2)], 0)) for b in range(B)]

    oT = mlp_launch(1, True, x1T, GT2, hy_w_out[0], hy_b_out[0])
    out = np.stack([_C(oT[b].T) for b in range(B)], 0).astype(np.float32)
    return out
```
